# Optimizing a Trainium2 kernel written in Bass

```python
import math
import jax, jax.numpy as jnp
from jax import lax
import numpy as np

D_MODEL = 2048
BATCH = 2
SEQ = 16384
DEPTH = 2

CHUNK = 64
RMS_EPS = 1e-6
D_FF = 5632
RWKV_HEADS = 12
RWKV_HEAD_DIM = 64
RWKV_DIM = RWKV_HEADS * RWKV_HEAD_DIM
DECAY_LORA = 64
ICLR_LORA = 64
GATE_LORA = 128
RWKV_GN_EPS = 64e-5
GDN_HEADS = 6
GDN_HEAD_DIM = 128
GDN_DIM = GDN_HEADS * GDN_HEAD_DIM
GDN_CONV = 4
LRU_BLOCKS = 8
LRU_BLOCK_DIM = 64
LRU_DIM = LRU_BLOCKS * LRU_BLOCK_DIM
LRU_CONV = 4
LRU_C = 8.0
MIX_DIM = RWKV_DIM + GDN_DIM + LRU_DIM
RWKV_IN = 3 * RWKV_DIM + DECAY_LORA + ICLR_LORA + GATE_LORA
GDN_IN = 4 * GDN_DIM + 2 * GDN_HEADS
LRU_IN = 2 * LRU_DIM
PROJ_DIM = RWKV_IN + GDN_IN + LRU_IN

kernel_name = "hybrid_rwkv7_gdn_rglru_macaron"


def rmsnorm(x, g):
    xf = x.astype(jnp.float32)
    y = xf * lax.rsqrt(jnp.mean(xf * xf, axis=-1, keepdims=True) + RMS_EPS)
    return (y * g.astype(jnp.float32)).astype(x.dtype)


def l2norm(x):
    xf = x.astype(jnp.float32)
    return xf * lax.rsqrt(jnp.sum(xf * xf, axis=-1, keepdims=True) + 1e-12)


def swiglu(h, wg, wu, wd):
    return (jax.nn.silu(h @ wg) * (h @ wu)) @ wd


def causal_dwconv(x, w):
    K, C = w.shape
    return lax.conv_general_dilated(x, w[:, None, :].astype(x.dtype), window_strides=(1,),
                                    padding=[(K - 1, 0)],
                                    dimension_numbers=('NWC', 'WIO', 'NWC'),
                                    feature_group_count=C)


def rwkv7_recurrence(r, w, k, v, kk, a):
    B, T, H, N = r.shape

    def step(S, inp):
        r_t, w_t, k_t, v_t, kk_t, a_t = inp
        sa = jnp.einsum('bhvk,bhk->bhv', S, kk_t)
        S = (S * w_t[:, :, None, :]
             - sa[..., None] * (kk_t * a_t)[:, :, None, :]
             + v_t[..., None] * k_t[:, :, None, :])
        return S, jnp.einsum('bhvk,bhk->bhv', S, r_t)

    xs = tuple(jnp.swapaxes(t, 0, 1) for t in (r, w, k, v, kk, a))
    _, y = lax.scan(step, jnp.zeros((B, H, N, N), jnp.float32), xs)
    return jnp.swapaxes(y, 0, 1)


def rwkv7_mixer(z, mu, w0, w2, a0, a2, g2, k_k, k_a, r_k, ln_g, ln_b):
    B, T, _ = z.shape
    zf = z.astype(jnp.float32)
    z_prev = jnp.pad(zf[:, :-1], ((0, 0), (1, 0), (0, 0)))
    zf = zf + mu * (z_prev - zf)
    o1 = RWKV_DIM
    o2 = 2 * RWKV_DIM
    o3 = 3 * RWKV_DIM
    o4 = o3 + DECAY_LORA
    o5 = o4 + ICLR_LORA
    r, k, v, zw, za, zg = jnp.split(zf, [o1, o2, o3, o4, o5], axis=-1)
    w_log = -jax.nn.softplus(-(w0 + jnp.tanh(zw) @ w2)) - 0.5
    decay = jnp.exp(-jnp.exp(w_log))
    a = jax.nn.sigmoid(a0 + za @ a2)
    g = jax.nn.sigmoid(zg) @ g2
    heads = lambda t: t.reshape(B, T, RWKV_HEADS, RWKV_HEAD_DIM)
    kk = l2norm(heads(k * k_k))
    k = k * (1.0 + (a - 1.0) * k_a)
    r, k, v, decay, a = map(heads, (r, k, v, decay, a))
    y = rwkv7_recurrence(r, decay, k, v, kk, a)
    mean = jnp.mean(y, axis=-1, keepdims=True)
    var = jnp.mean(jnp.square(y - mean), axis=-1, keepdims=True)
    y = ((y - mean) * lax.rsqrt(var + RWKV_GN_EPS)).reshape(B, T, RWKV_DIM) * ln_g + ln_b
    bonus = jnp.sum(r * k * r_k, axis=-1, keepdims=True) * v
    y = (y + bonus.reshape(B, T, RWKV_DIM)) * g
    return y.astype(z.dtype)


def gated_delta_chunked(q, k, v, g, beta):
    B, T, H, DK = q.shape
    DV = v.shape[-1]
    n = T // CHUNK

    def blocks(t):
        t = t.reshape((B, n, CHUNK, H) + t.shape[3:])
        return jnp.moveaxis(t, 3, 1)

    q, k, v, g, beta = map(blocks, (q, k, v, g, beta))
    gc = jnp.cumsum(g, axis=-1)
    idx = jnp.arange(CHUNK)
    causal = idx[:, None] >= idx[None, :]
    strict = idx[:, None] > idx[None, :]
    diff = gc[..., :, None] - gc[..., None, :]
    decay = jnp.where(causal, jnp.exp(jnp.where(causal, diff, 0.0)), 0.0)
    kb = k * beta[..., None]
    m = jnp.where(strict, jnp.einsum('bhnid,bhnjd->bhnij', kb, k) * decay, 0.0)
    lower = m + jnp.eye(CHUNK, dtype=m.dtype)
    rhs = jnp.concatenate([v * beta[..., None], kb * jnp.exp(gc)[..., None]], axis=-1)
    sol = lax.linalg.triangular_solve(lower, rhs, left_side=True, lower=True,
                                      unit_diagonal=True)
    u, wk = sol[..., :DV], sol[..., DV:]
    qk = jnp.where(causal, jnp.einsum('bhnid,bhnjd->bhnij', q, k) * decay, 0.0)

    def step(S, inp):
        q_i, k_i, u_i, w_i, qk_i, gc_i = inp
        v_new = u_i - jnp.einsum('bhcd,bhde->bhce', w_i, S)
        o = (jnp.einsum('bhcd,bhde->bhce', q_i * jnp.exp(gc_i)[..., None], S)
             + jnp.einsum('bhij,bhje->bhie', qk_i, v_new))
        g_last = gc_i[..., -1:]
        S = (S * jnp.exp(g_last)[..., None]
             + jnp.einsum('bhcd,bhce->bhde', k_i * jnp.exp(g_last - gc_i)[..., None], v_new))
        return S, o

    xs = tuple(jnp.moveaxis(t, 2, 0) for t in (q, k, u, wk, qk, gc))
    _, o = lax.scan(step, jnp.zeros((B, H, DK, DV), jnp.float32), xs)
    o = jnp.transpose(o, (1, 0, 3, 2, 4))
    return o.reshape(B, T, H, DV)


def gdn_mixer(z, conv_w, a_log, dt_bias, norm_g):
    B, T, _ = z.shape
    qkv, gate, zb, za = jnp.split(z, [3 * GDN_DIM, 4 * GDN_DIM, 4 * GDN_DIM + GDN_HEADS], axis=-1)
    qkv = jax.nn.silu(causal_dwconv(qkv, conv_w)).astype(jnp.float32)
    q, k, v = jnp.split(qkv.reshape(B, T, 3 * GDN_HEADS, GDN_HEAD_DIM), 3, axis=2)
    q = l2norm(q) * (GDN_HEAD_DIM ** -0.5)
    k = l2norm(k)
    beta = jax.nn.sigmoid(zb.astype(jnp.float32))
    g = -jnp.exp(a_log.astype(jnp.float32)) * jax.nn.softplus(za.astype(jnp.float32) + dt_bias)
    o = gated_delta_chunked(q, k, v, g, beta)
    o = rmsnorm(o, norm_g) * jax.nn.silu(gate.astype(jnp.float32).reshape(B, T, GDN_HEADS, GDN_HEAD_DIM))
    return o.reshape(B, T, GDN_DIM).astype(z.dtype)


def rglru_mixer(z, conv_w, conv_b, w_a, b_a, w_x, b_x, lam):
    B, T, _ = z.shape
    xl, yl = jnp.split(z, 2, axis=-1)
    xc = (causal_dwconv(xl, conv_w) + conv_b).astype(jnp.float32)
    xb = xc.reshape(B, T, LRU_BLOCKS, LRU_BLOCK_DIM)
    r = jax.nn.sigmoid(jnp.einsum('btnc,ncd->btnd', xb, w_a).reshape(B, T, LRU_DIM) + b_a)
    i = jax.nn.sigmoid(jnp.einsum('btnc,ncd->btnd', xb, w_x).reshape(B, T, LRU_DIM) + b_x)
    log_a = -LRU_C * r * jax.nn.softplus(-lam.astype(jnp.float32))
    a = jnp.exp(log_a)
    u = jnp.sqrt(-jnp.expm1(2.0 * log_a)) * (i * xc)

    def combine(e, l):
        return (e[0] * l[0], l[0] * e[1] + l[1])

    _, h = lax.associative_scan(combine, (a, u), axis=1)
    return (h * jax.nn.gelu(yl.astype(jnp.float32))).astype(z.dtype)


def setup_inputs(seed: int = 0) -> dict:
    key = jax.random.key(seed)
    ks = iter(jax.random.split(key, 48))
    f32 = jnp.float32
    L = DEPTH

    def nrm(shape, scale):
        return jax.random.normal(next(ks), shape, f32) * scale

    def gain(shape):
        return 1.0 + nrm(shape, 0.02)

    def unif(shape, lo, hi):
        return jax.random.uniform(next(ks), shape, f32, lo, hi)

    x = nrm((BATCH, SEQ, D_MODEL), 1.0)
    norm1_g = gain((L, D_MODEL))
    ffn1_wg = nrm((L, D_MODEL, D_FF), D_MODEL ** -0.5)
    ffn1_wu = nrm((L, D_MODEL, D_FF), D_MODEL ** -0.5)
    ffn1_wd = nrm((L, D_FF, D_MODEL), D_FF ** -0.5)
    norm_mix_g = gain((L, D_MODEL))
    w_in = nrm((L, D_MODEL, PROJ_DIM), D_MODEL ** -0.5)
    rw_mu = unif((L, RWKV_IN), 0.0, 1.0)
    rw_w0 = unif((L, RWKV_DIM), -6.5, -1.5)
    rw_w2 = nrm((L, DECAY_LORA, RWKV_DIM), 0.5 * DECAY_LORA ** -0.5)
    rw_a0 = nrm((L, RWKV_DIM), 0.1)
    rw_a2 = nrm((L, ICLR_LORA, RWKV_DIM), 0.5 * ICLR_LORA ** -0.5)
    rw_g2 = nrm((L, GATE_LORA, RWKV_DIM), GATE_LORA ** -0.5)
    rw_kk = 0.85 + nrm((L, RWKV_DIM), 0.02)
    rw_ka = gain((L, RWKV_DIM))
    rw_rk = nrm((L, RWKV_HEADS, RWKV_HEAD_DIM), 0.1)
    rw_ln_g = gain((L, RWKV_DIM))
    rw_ln_b = nrm((L, RWKV_DIM), 0.02)
    gd_conv_w = nrm((L, GDN_CONV, 3 * GDN_DIM), GDN_CONV ** -0.5)
    gd_a_log = jnp.log(unif((L, GDN_HEADS), 1.0, 16.0))
    dt = jnp.exp(unif((L, GDN_HEADS), math.log(1e-3), math.log(1e-1)))
    gd_dt_bias = dt + jnp.log(-jnp.expm1(-dt))
    gd_norm_g = gain((L, GDN_HEAD_DIM))
    lr_conv_w = nrm((L, LRU_CONV, LRU_DIM), LRU_CONV ** -0.5)
    lr_conv_b = nrm((L, LRU_DIM), 0.02)
    lr_wa = nrm((L, LRU_BLOCKS, LRU_BLOCK_DIM, LRU_BLOCK_DIM), LRU_BLOCK_DIM ** -0.5)
    lr_ba = nrm((L, LRU_DIM), 0.02)
    lr_wx = nrm((L, LRU_BLOCKS, LRU_BLOCK_DIM, LRU_BLOCK_DIM), LRU_BLOCK_DIM ** -0.5)
    lr_bx = nrm((L, LRU_DIM), 0.02)
    s = unif((L, LRU_DIM), 0.9, 0.999) ** (1.0 / LRU_C)
    lr_lam = jnp.log(s) - jnp.log1p(-s)
    w_out = nrm((L, MIX_DIM, D_MODEL), MIX_DIM ** -0.5)
    norm2_g = gain((L, D_MODEL))
    ffn2_wg = nrm((L, D_MODEL, D_FF), D_MODEL ** -0.5)
    ffn2_wu = nrm((L, D_MODEL, D_FF), D_MODEL ** -0.5)
    ffn2_wd = nrm((L, D_FF, D_MODEL), D_FF ** -0.5)
    final_g = gain((D_MODEL,))
    return {"x": x, "norm1_g": norm1_g, "ffn1_wg": ffn1_wg, "ffn1_wu": ffn1_wu,
            "ffn1_wd": ffn1_wd, "norm_mix_g": norm_mix_g, "w_in": w_in,
            "rw_mu": rw_mu, "rw_w0": rw_w0, "rw_w2": rw_w2, "rw_a0": rw_a0,
            "rw_a2": rw_a2, "rw_g2": rw_g2, "rw_kk": rw_kk, "rw_ka": rw_ka,
            "rw_rk": rw_rk, "rw_ln_g": rw_ln_g, "rw_ln_b": rw_ln_b,
            "gd_conv_w": gd_conv_w, "gd_a_log": gd_a_log, "gd_dt_bias": gd_dt_bias,
            "gd_norm_g": gd_norm_g, "lr_conv_w": lr_conv_w, "lr_conv_b": lr_conv_b,
            "lr_wa": lr_wa, "lr_ba": lr_ba, "lr_wx": lr_wx, "lr_bx": lr_bx,
            "lr_lam": lr_lam, "w_out": w_out, "norm2_g": norm2_g,
            "ffn2_wg": ffn2_wg, "ffn2_wu": ffn2_wu, "ffn2_wd": ffn2_wd,
            "final_g": final_g}


def reference(x, norm1_g, ffn1_wg, ffn1_wu, ffn1_wd, norm_mix_g, w_in,
              rw_mu, rw_w0, rw_w2, rw_a0, rw_a2, rw_g2, rw_kk, rw_ka, rw_rk,
              rw_ln_g, rw_ln_b, gd_conv_w, gd_a_log, gd_dt_bias, gd_norm_g,
              lr_conv_w, lr_conv_b, lr_wa, lr_ba, lr_wx, lr_bx, lr_lam, w_out,
              norm2_g, ffn2_wg, ffn2_wu, ffn2_wd, final_g):
    for l in range(DEPTH):
        x = x + 0.5 * swiglu(rmsnorm(x, norm1_g[l]), ffn1_wg[l], ffn1_wu[l], ffn1_wd[l])
        h = rmsnorm(x, norm_mix_g[l])
        z = h @ w_in[l]
        z_rw, z_gd, z_lr = jnp.split(z, [RWKV_IN, RWKV_IN + GDN_IN], axis=-1)
        y_rw = rwkv7_mixer(z_rw, rw_mu[l], rw_w0[l], rw_w2[l], rw_a0[l], rw_a2[l], rw_g2[l],
                           rw_kk[l], rw_ka[l], rw_rk[l], rw_ln_g[l], rw_ln_b[l])
        y_gd = gdn_mixer(z_gd, gd_conv_w[l], gd_a_log[l], gd_dt_bias[l], gd_norm_g[l])
        y_lr = rglru_mixer(z_lr, lr_conv_w[l], lr_conv_b[l], lr_wa[l], lr_ba[l],
                           lr_wx[l], lr_bx[l], lr_lam[l])
        mix = jnp.concatenate([y_rw, y_gd, y_lr], axis=-1)
        x = x + mix @ w_out[l]
        x = x + 0.5 * swiglu(rmsnorm(x, norm2_g[l]), ffn2_wg[l], ffn2_wu[l], ffn2_wd[l])
    return rmsnorm(x, final_g)
```

```python
import contextlib
import numpy as np
import concourse.bass as bass
import concourse.mybir as mybir
from concourse.bass_utils import run_bass_kernel_spmd

F32 = mybir.dt.float32
BF16 = mybir.dt.bfloat16
AF = mybir.ActivationFunctionType
ALU = mybir.AluOpType
AX = mybir.AxisListType

D_MODEL = 2048
D_FF = 5632
PROJ = 6668
NCORES = 8
RMS_EPS = 1e-6

ENGS = ("pe", "act", "dve", "pool", "sp")


class Prog:
    def __init__(self, nc):
        self.nc = nc
        self.st = contextlib.ExitStack()
        self.ops = {e: [] for e in ENGS}
        self.cnt = {e: 0 for e in ENGS}
        self.known = {e: {} for e in ENGS}
        self.last_w = {}
        self.readers = {}
        self.lane_cnt = {}
        self.sems = {}
        self.final_tokens = []

    def sb(self, name, shape, dtype):
        return self.st.enter_context(self.nc.sbuf_tensor(name, list(shape), dtype))

    def ps(self, name, shape, dtype=F32):
        return self.st.enter_context(self.nc.psum_tensor(name, list(shape), dtype))

    def _deps(self, reads, writes):
        deps = []
        for r in reads:
            t = self.last_w.get(r)
            if t is not None:
                deps.append(t)
        for w in writes:
            t = self.last_w.get(w)
            if t is not None:
                deps.append(t)
            deps.extend(self.readers.get(w, ()))
        return deps

    def _commit(self, tok, reads, writes):
        for r in reads:
            self.readers.setdefault(r, []).append(tok)
        for w in writes:
            self.last_w[w] = tok
            self.readers[w] = []

    def _filter(self, eng, deps):
        best = {}
        for (sk, v) in deps:
            if eng == "pe" and sk == "pe":
                continue
            if v > best.get(sk, 0):
                best[sk] = v
        out = []
        kn = self.known[eng]
        for sk, v in best.items():
            if kn.get(sk, 0) >= v:
                continue
            kn[sk] = v
            out.append((sk, v))
        return out

    def op(self, eng, fn, reads=(), writes=()):
        waits = self._filter(eng, self._deps(reads, writes))
        self.cnt[eng] += 1
        tok = (eng, self.cnt[eng])
        self.ops[eng].append((waits, fn, tok))
        self._commit(tok, reads, writes)
        return tok

    def dma(self, eng, lane, fn, reads=(), writes=()):
        waits = self._filter(eng, self._deps(reads, writes))
        lk = ("lane", lane)
        self.lane_cnt[lk] = self.lane_cnt.get(lk, 0) + 16
        tok = (lk, self.lane_cnt[lk])
        self.ops[eng].append((waits, fn, tok))
        self._commit(tok, reads, writes)
        return tok

    def finish(self, tokens):
        self.final_tokens.extend(tokens)

    def emit(self):
        nc = self.nc
        st = self.st
        semkeys = list(ENGS[:4]) + sorted(self.lane_cnt.keys(), key=str)
        for i, sk in enumerate(semkeys):
            self.sems[sk] = st.enter_context(nc.semaphore("s%d" % i))
        block = st.enter_context(nc.Block())

        ref = {e: set() for e in ENGS[:4]}
        for e in ENGS:
            for (waits, fn, tok) in self.ops[e]:
                for (sk, v) in waits:
                    if not isinstance(sk, tuple):
                        ref[sk].add(v)
        rank = {e: {v: i + 1 for i, v in enumerate(sorted(ref[e]))} for e in ref}

        def run(engname, engobj):
            for (waits, fn, tok) in self.ops[engname]:
                for (sk, v) in waits:
                    engobj.wait_ge(self.sems[sk], v if isinstance(sk, tuple) else rank[sk][v])
                ins = fn(engobj)
                sk, v = tok
                if isinstance(sk, tuple):
                    ins.then_inc(self.sems[sk], 16)
                elif v in rank[sk]:
                    ins.then_inc(self.sems[sk], 1)
            if engname == "sp":
                best = {}
                for (sk, v) in self.final_tokens:
                    best[sk] = max(best.get(sk, 0), v)
                for sk, v in best.items():
                    engobj.wait_ge(self.sems[sk], v)

        @block.tensor
        def _(e):
            run("pe", e)

        @block.scalar
        def _(e):
            run("act", e)

        @block.vector
        def _(e):
            run("dve", e)

        @block.gpsimd
        def _(e):
            run("pool", e)

        @block.sync
        def _(e):
            run("sp", e)

        st.close()


class Dense:
    def __init__(self, Tc, stages, MT=1024, D=D_MODEL, F=D_FF, PJ=PROJ):
        self.Tc, self.MT, self.D, self.F, self.PJ = Tc, MT, D, F, PJ
        self.stages = stages
        self.KC = D // 128
        self.NTT = MT // 512
        nc = self.nc = bass.Bass("TRN2", target_bir_lowering=False)
        P = self.P = Prog(nc)
        KC, NTT = self.KC, self.NTT
        self.xT = nc.dram_tensor("xT", [D, Tc], F32, kind="ExternalInput").ap()
        self.oT = nc.dram_tensor("oT", [D, Tc], F32, kind="ExternalOutput").ap()
        self.dr = {}
        for si, s in enumerate(stages):
            k = s[0]
            if k == "wout":
                self.dr[(si, "mixT")] = nc.dram_tensor("mixT", [D, Tc], F32, kind="ExternalInput").ap()
                self.dr[(si, "w")] = nc.dram_tensor("wout%d" % si, [D, D], F32, kind="ExternalInput").ap()
            elif k == "ffn":
                self.dr[(si, "g")] = nc.dram_tensor("g%d" % si, [128, KC], F32, kind="ExternalInput").ap()
                self.dr[(si, "wg")] = nc.dram_tensor("wg%d" % si, [D, F], F32, kind="ExternalInput").ap()
                self.dr[(si, "wu")] = nc.dram_tensor("wu%d" % si, [D, F], F32, kind="ExternalInput").ap()
                self.dr[(si, "wd")] = nc.dram_tensor("wd%d" % si, [F, D], F32, kind="ExternalInput").ap()
            elif k == "win":
                self.dr[(si, "g")] = nc.dram_tensor("g%d" % si, [128, KC], F32, kind="ExternalInput").ap()
                self.dr[(si, "w")] = nc.dram_tensor("win%d" % si, [D, PJ], F32, kind="ExternalInput").ap()
                self.zT = nc.dram_tensor("zT", [PJ, Tc], F32, kind="ExternalOutput").ap()
            elif k == "final":
                self.dr[(si, "g")] = nc.dram_tensor("g%d" % si, [128, KC], F32, kind="ExternalInput").ap()
        self.xs = P.sb("xs", [128, KC, MT], F32)
        self.h = P.sb("h", [128, KC, MT], BF16)
        self.GRP = 11 if F % (11 * 128) == 0 else F // 128
        self.act = P.sb("act", [128, self.GRP, MT], BF16)
        self.NS, self.NB = 3, 3
        self.wst = [P.sb("wst%d" % i, [128, 8, 256], F32) for i in range(self.NS)]
        self.wbf = [P.sb("wbf%d" % i, [128, 16, 256], BF16) for i in range(self.NB)]
        self.scr = [P.sb("scr%d" % i, [128, MT], F32) for i in range(2)]
        self.sq = [P.sb("sq%d" % i, [128, MT], BF16) for i in range(2)]
        self.sil = [P.sb("sil%d" % i, [128, 512], F32) for i in range(2)]
        self.rstd = P.sb("rstd", [128, MT], F32)
        self.zst = [P.sb("zst%d" % i, [128, 512], F32) for i in range(3)]
        self.ones = P.sb("ones", [128, 128], BF16)
        self.gs = {}
        for si, s in enumerate(stages):
            if (si, "g") in self.dr:
                self.gs[si] = P.sb("gs%d" % si, [128, KC], F32)
        self.psum = P.ps("psum", [128, 8 * 512], F32)
        self.ctr = {"wst": 0, "wbf": 0, "scr": 0, "sq": 0, "sil": 0, "zst": 0, "gu": 0, "dn": 0, "lane": 0}
        self.out_tokens = []
        self._build()
        P.finish(self.out_tokens)
        P.emit()

    def bank(self, b):
        return self.psum[:, b * 512:(b + 1) * 512]

    def nxt(self, k, n):
        v = self.ctr[k] % n
        self.ctr[k] += 1
        return v

    def lane(self, base, n):
        v = self.ctr["lane"] % n
        self.ctr["lane"] += 1
        return "%s%d" % (base, v)

    def load_wtile(self, wap, k0, nk, n0, ncols):
        P = self.P
        b = self.nxt("wbf", self.NB)
        wb = self.wbf[b]
        src = wap.rearrange("(kc p) n -> p kc n", p=128)
        for u0 in range(0, nk, 8):
            un = min(8, nk - u0)
            s = self.nxt("wst", self.NS)
            ws = self.wst[s]
            P.dma("sp", "w%d" % s,
                  lambda e, ws=ws, u0=u0, un=un: e.dma_start(
                      out=ws[:, 0:un, 0:ncols], in_=src[:, k0 + u0:k0 + u0 + un, n0:n0 + ncols]),
                  writes=[("wst", s)])
            P.op("pool",
                 lambda e, ws=ws, u0=u0, un=un: e.tensor_copy(
                     out=wb[:, u0:u0 + un, 0:ncols], in_=ws[:, 0:un, 0:ncols]),
                 reads=[("wst", s)], writes=[("wbf", b)])
        return wb, b

    def norm(self, si, out_h=True):
        P, KC, NTT, MT = self.P, self.KC, self.NTT, self.MT
        g = self.gs[si]
        for fc in range(KC):
            q = self.nxt("sq", 2)
            sq = self.sq[q]
            P.op("act", lambda e, sq=sq, fc=fc: e.activation(out=sq[:], in_=self.xs[:, fc, :], func=AF.Square),
                 reads=[("xs", fc)], writes=[("sq", q)])
            for tt in range(NTT):
                P.op("pe", lambda e, sq=sq, fc=fc, tt=tt: e.matmul(
                    self.bank(6 + tt), lhsT=self.ones[:], rhs=sq[:, tt * 512:(tt + 1) * 512],
                    start=(fc == 0), stop=(fc == KC - 1)),
                    reads=[("sq", q), "ones"], writes=[("bank", 6 + tt)])
        s = self.nxt("scr", 2)
        scr = self.scr[s]
        for tt in range(NTT):
            P.op("act", lambda e, tt=tt: e.activation(
                out=scr[:, tt * 512:(tt + 1) * 512], in_=self.bank(6 + tt), func=AF.Sqrt,
                scale=1.0 / self.D, bias=self.epsb[:, 0:1]),
                reads=[("bank", 6 + tt), "epsb"], writes=[("scr", s)])
        P.op("dve", lambda e: e.reciprocal(out=self.rstd[:], in_=scr[:]),
             reads=[("scr", s)], writes=["rstd"])
        if out_h:
            for fc in range(KC):
                P.op("dve", lambda e, fc=fc: e.scalar_tensor_tensor(
                    out=self.h[:, fc, :], in0=self.xs[:, fc, :], scalar=g[:, fc:fc + 1], in1=self.rstd[:],
                    op0=ALU.mult, op1=ALU.mult),
                    reads=[("xs", fc), "rstd", ("gs", si)], writes=[("h", fc)])

    def proj_to_x(self, wap, k0, nk, rhs_of_k, rhs_res_of_k, scale):
        P, KC, NTT = self.P, self.KC, self.NTT
        for oc2 in range(0, KC, 2):
            wb, b = self.load_wtile(wap, k0, nk, oc2 * 128, 256)
            for j in range(2):
                oc = oc2 + j
                for tt in range(NTT):
                    bk = 4 + self.nxt("dn", 2)
                    for k in range(nk):
                        P.op("pe", lambda e, wb=wb, k=k, j=j, tt=tt, bk=bk: e.matmul(
                            self.bank(bk), lhsT=wb[:, k, j * 128:(j + 1) * 128],
                            rhs=rhs_of_k(k, tt), start=(k == 0), stop=(k == nk - 1)),
                            reads=[("wbf", b), rhs_res_of_k(k)], writes=[("bank", bk)])
                    P.op("dve", lambda e, oc=oc, tt=tt, bk=bk: e.scalar_tensor_tensor(
                        out=self.xs[:, oc, tt * 512:(tt + 1) * 512], in0=self.bank(bk), scalar=float(scale),
                        in1=self.xs[:, oc, tt * 512:(tt + 1) * 512], op0=ALU.mult, op1=ALU.add),
                        reads=[("bank", bk), ("xs", oc)], writes=[("xs", oc)])

    def _build(self):
        P, KC, NTT, MT, D, F = self.P, self.KC, self.NTT, self.MT, self.D, self.F
        P.op("pool", lambda e: e.memset(self.ones[:], 1.0), writes=["ones"])
        self.epsb = P.sb("epsb", [128, 1], F32)
        P.op("pool", lambda e: e.memset(self.epsb[:], RMS_EPS), writes=["epsb"])
        for si in self.gs:
            P.dma("act", "g%d" % si, lambda e, si=si: e.dma_start(out=self.gs[si][:], in_=self.dr[(si, "g")][:, :]),
                  writes=[("gs", si)])
        xTv = self.xT.rearrange("(fc p) t -> p fc t", p=128)
        oTv = self.oT.rearrange("(fc p) t -> p fc t", p=128)
        for mt in range(self.Tc // MT):
            t0 = mt * MT
            for q in range(4):
                fcs = slice(q * KC // 4, (q + 1) * KC // 4)
                P.dma("act", "x%d" % q, lambda e, fcs=fcs, t0=t0: e.dma_start(
                    out=self.xs[:, fcs, :], in_=xTv[:, fcs, t0:t0 + MT]),
                    writes=[("xs", fc) for fc in range(fcs.start, fcs.stop)])
            for si, s in enumerate(self.stages):
                kind = s[0]
                if kind == "wout":
                    mixv = self.dr[(si, "mixT")].rearrange("(fc p) t -> p fc t", p=128)
                    for fc in range(KC):
                        c = self.nxt("scr", 2)
                        scr = self.scr[c]
                        P.dma("act", "m%d" % c, lambda e, scr=scr, fc=fc, t0=t0: e.dma_start(
                            out=scr[:], in_=mixv[:, fc, t0:t0 + MT]), writes=[("scr", c)])
                        P.op("pool", lambda e, scr=scr, fc=fc: e.tensor_copy(out=self.h[:, fc, :], in_=scr[:]),
                             reads=[("scr", c)], writes=[("h", fc)])
                    self.proj_to_x(self.dr[(si, "w")], 0, KC,
                                   lambda k, tt: self.h[:, k, tt * 512:(tt + 1) * 512],
                                   lambda k: ("h", k), 1.0)
                elif kind == "ffn":
                    self.norm(si)
                    wg, wu, wd = self.dr[(si, "wg")], self.dr[(si, "wu")], self.dr[(si, "wd")]
                    GRP = self.GRP
                    for grp in range(F // 128 // GRP):
                        nb = grp * GRP
                        jj = 0
                        while jj < GRP:
                            nw = min(2, GRP - jj)
                            n0 = (nb + jj) * 128
                            wgb, bg = self.load_wtile(wg, 0, KC, n0, nw * 128)
                            wub, bu = self.load_wtile(wu, 0, KC, n0, nw * 128)
                            for j in range(nw):
                                for tt in range(NTT):
                                    pb = 2 * self.nxt("gu", 2)
                                    for (wb_, b_, bk) in ((wgb, bg, pb), (wub, bu, pb + 1)):
                                        for k in range(KC):
                                            P.op("pe", lambda e, wb_=wb_, k=k, j=j, tt=tt, bk=bk: e.matmul(
                                                self.bank(bk), lhsT=wb_[:, k, j * 128:(j + 1) * 128],
                                                rhs=self.h[:, k, tt * 512:(tt + 1) * 512],
                                                start=(k == 0), stop=(k == KC - 1)),
                                                reads=[("wbf", b_), ("h", k)], writes=[("bank", bk)])
                                    c = self.nxt("sil", 2)
                                    sil = self.sil[c]
                                    P.op("act", lambda e, sil=sil, pb=pb: e.activation(
                                        out=sil[:], in_=self.bank(pb), func=AF.Silu),
                                        reads=[("bank", pb)], writes=[("sil", c)])
                                    P.op("dve", lambda e, sil=sil, pb=pb, a=jj + j, tt=tt: e.tensor_tensor(
                                        out=self.act[:, a, tt * 512:(tt + 1) * 512], in0=sil[:],
                                        in1=self.bank(pb + 1), op=ALU.mult),
                                        reads=[("sil", c), ("bank", pb + 1)], writes=[("act", jj + j)])
                            jj += nw
                        self.proj_to_x(wd, nb, GRP,
                                       lambda k, tt: self.act[:, k, tt * 512:(tt + 1) * 512],
                                       lambda k: ("act", k), 0.5)
                elif kind == "win":
                    self.norm(si)
                    w = self.dr[(si, "w")]
                    PJ = self.PJ
                    n0 = 0
                    while n0 < PJ:
                        ncols = min(256, PJ - n0)
                        wb, b = self.load_wtile(w, 0, KC, n0, ncols)
                        for j0 in range(0, ncols, 128):
                            m = min(128, ncols - j0)
                            for tt in range(NTT):
                                bk = 4 + self.nxt("dn", 2)
                                for k in range(KC):
                                    P.op("pe", lambda e, wb=wb, k=k, j0=j0, m=m, tt=tt, bk=bk: e.matmul(
                                        self.bank(bk)[0:m, :], lhsT=wb[:, k, j0:j0 + m],
                                        rhs=self.h[:, k, tt * 512:(tt + 1) * 512],
                                        start=(k == 0), stop=(k == KC - 1)),
                                        reads=[("wbf", b), ("h", k)], writes=[("bank", bk)])
                                c = self.nxt("zst", 3)
                                zs = self.zst[c]
                                P.op("act", lambda e, zs=zs, bk=bk, m=m: e.activation(
                                    out=zs[0:m, :], in_=self.bank(bk)[0:m, :], func=AF.Copy),
                                    reads=[("bank", bk)], writes=[("zst", c)])
                                r0 = n0 + j0
                                tk = P.dma("act", "z%d" % c, lambda e, zs=zs, r0=r0, m=m, tt=tt, t0=t0: e.dma_start(
                                    out=self.zT[r0:r0 + m, t0 + tt * 512:t0 + (tt + 1) * 512], in_=zs[0:m, :]),
                                    reads=[("zst", c)])
                                self.out_tokens.append(tk)
                        n0 += ncols
                elif kind == "final":
                    self.norm(si, out_h=False)
                    g = self.gs[si]
                    for fc in range(KC):
                        P.op("dve", lambda e, fc=fc, g=g: e.scalar_tensor_tensor(
                            out=self.xs[:, fc, :], in0=self.xs[:, fc, :], scalar=g[:, fc:fc + 1], in1=self.rstd[:],
                            op0=ALU.mult, op1=ALU.mult),
                            reads=[("xs", fc), "rstd", ("gs", si)], writes=[("xs", fc)])
            for q in range(4):
                fcs = slice(q * KC // 4, (q + 1) * KC // 4)
                tk = P.dma("act", "o%d" % q, lambda e, fcs=fcs, t0=t0: e.dma_start(
                    out=oTv[:, fcs, t0:t0 + MT], in_=self.xs[:, fcs, :]),
                    reads=[("xs", fc) for fc in range(fcs.start, fcs.stop)])
                self.out_tokens.append(tk)


def g_pm(g):
    return np.ascontiguousarray(g.reshape(-1, 128).T)


CH = 128
DEC_C = 0.6065306597126334
GN_EPS = 64e-5


class Mixer:
    STAGE_LIMIT = None
    INTERLEAVE = True
    def __init__(self, T, TB=1024, do_rw=True, do_gd=True, do_lr=True, n_rw=3, n_gd=2):
        self.T, self.TB = T, TB
        self.NCH = TB // CH
        nc = self.nc = bass.Bass("TRN2", target_bir_lowering=False)
        P = self.P = Prog(nc)
        self.n_rw = n_rw if do_rw else 0
        self.n_gd = n_gd if do_gd else 0
        self.do_lr = do_lr
        dt = nc.dram_tensor
        self.d_const = dt("consts", [128, 128 + 128 + 512 + TB], F32, kind="ExternalInput").ap()
        if self.n_rw:
            self.d_rwx = dt("rw_x", [n_rw * 3, 64, T + 1], F32, kind="ExternalInput").ap()
            self.d_rwl = dt("rw_l", [256, T + 1], F32, kind="ExternalInput").ap()
            self.d_rwp = dt("rw_p", [64, n_rw * 10 + 2], F32, kind="ExternalInput").ap()
            self.d_rwpg = dt("rw_pg", [128, 1], F32, kind="ExternalInput").ap()
            self.d_rww2 = dt("rw_w2", [64, n_rw, 64], F32, kind="ExternalInput").ap()
            self.d_rwa2 = dt("rw_a2", [64, n_rw, 64], F32, kind="ExternalInput").ap()
            self.d_rwg2 = dt("rw_g2", [128, n_rw, 64], F32, kind="ExternalInput").ap()
            self.d_rwo = dt("mix_rw", [n_rw, 64, T], F32, kind="ExternalOutput").ap()
        if self.n_gd:
            self.d_gdx = dt("gd_x", [n_gd * 4, 128, T + 3], F32, kind="ExternalInput").ap()
            self.d_gdcw = dt("gd_cw", [128, n_gd * 12], F32, kind="ExternalInput").ap()
            self.d_gds = dt("gd_s", [n_gd * 2, 128, T // CH], F32, kind="ExternalInput").ap()
            self.d_gdsr = dt("gd_sr", [n_gd, 1, T], F32, kind="ExternalInput").ap()
            self.d_gdp = dt("gd_p", [128, n_gd * 3], F32, kind="ExternalInput").ap()
            self.d_gdo = dt("mix_gd", [n_gd, 128, T], F32, kind="ExternalOutput").ap()
        if do_lr:
            self.d_lrx = dt("lr_x", [2, 128, T + 3], F32, kind="ExternalInput").ap()
            self.d_lrw = dt("lr_w", [128, 2, 128], F32, kind="ExternalInput").ap()
            self.d_lrp = dt("lr_p", [128, 8], F32, kind="ExternalInput").ap()
            self.d_lro = dt("mix_lr", [128, T], F32, kind="ExternalOutput").ap()
        self.cst = P.sb("cst", [128, 128 + 128 + 512 + TB], F32)
        self.ident = self.cst[:, 0:128]
        self.ones = self.cst[:, 128:256]
        self.mask4 = self.cst[:, 256:768]
        self.rowm = self.cst[:, 768:768 + TB]
        self.NSCR = 54
        self.S = [P.sb("S%d" % i, [128, TB + 4], F32) for i in range(self.NSCR)]
        self.psum = P.ps("psum", [128, 8 * 512], F32)
        self.small = P.sb("small", [128, 64], F32)
        self.ctr = {}
        self.out_tokens = []
        self.units = []
        self._setup()
        for blk in range(T // TB):
            self._block(blk)
        P.finish(self.out_tokens)
        P.emit()

    def nxt(self, k, n):
        v = self.ctr.get(k, 0)
        self.ctr[k] = v + 1
        return v % n

    def reg(self, bank, r, w=128):
        return self.psum[:, bank * 512 + r * 128: bank * 512 + r * 128 + w]

    def tt(self, eng, out, in0, in1, op, R, W):
        return self.P.op(eng, lambda e: e.tensor_tensor(out=out, in0=in0, in1=in1, op=op), R, W)

    def ts(self, eng, out, in0, s1, s2, op0, op1, R, W):
        if s2 is None:
            return self.P.op(eng, lambda e: e.tensor_scalar(out=out, in0=in0, scalar1=s1, scalar2=None, op0=op0), R, W)
        return self.P.op(eng, lambda e: e.tensor_scalar(out=out, in0=in0, scalar1=s1, scalar2=s2, op0=op0, op1=op1), R, W)

    def stt(self, out, in0, sc, in1, op0, op1, R, W):
        return self.P.op("dve", lambda e: e.scalar_tensor_tensor(out=out, in0=in0, scalar=sc, in1=in1, op0=op0, op1=op1), R, W)

    def av(self, out, in_, func, R, W, scale=1.0, bias=None):
        if bias is None:
            return self.P.op("act", lambda e: e.activation(out=out, in_=in_, func=func, scale=scale), R, W)
        return self.P.op("act", lambda e: e.activation(out=out, in_=in_, func=func, scale=scale, bias=bias), R, W)

    def mm(self, out, lhsT, rhs, R, W, start=True, stop=True):
        return self.P.op("pe", lambda e: e.matmul(out, lhsT=lhsT, rhs=rhs, start=start, stop=stop), R, W)

    def rcp(self, out, in_, R, W):
        return self.P.op("dve", lambda e: e.reciprocal(out=out, in_=in_), R, W)

    def ld(self, lane, out, in_, W, eng="sp"):
        return self.P.dma(eng, lane, lambda e: e.dma_start(out=out, in_=in_), (), W)

    def st(self, lane, out, in_, R, eng="sp"):
        tk = self.P.dma(eng, lane, lambda e: e.dma_start(out=out, in_=in_), R, ())
        self.out_tokens.append(tk)
        return tk

    def _setup(self):
        P = self.P
        self.ld("c0", self.cst[:, :], self.d_const[:, :], ["cst"])
        sm = self.small
        P.op("pool", lambda e: e.memset(sm[:, 0:1], 1e-12), (), [("small", 0)])
        P.op("pool", lambda e: e.memset(sm[:, 1:2], GN_EPS), (), [("small", 1)])
        P.op("pool", lambda e: e.memset(sm[:, 2:3], RMS_EPS), (), [("small", 2)])
        P.op("pool", lambda e: e.memset(sm[:, 3:4], 1.0), (), [("small", 3)])
        self.eps12, self.epsgn, self.epsrms, self.one1 = sm[:, 0:1], sm[:, 1:2], sm[:, 2:3], sm[:, 3:4]
        uid = 0
        if self.n_rw:
            n = self.n_rw
            self.rwp = P.sb("rwp", [64, n * 10 + 3], F32)
            self.rwpg = P.sb("rwpg", [128, 1], F32)
            self.rww2 = P.sb("rww2", [64, n, 64], F32)
            self.rwa2 = P.sb("rwa2", [64, n, 64], F32)
            self.rwg2 = P.sb("rwg2", [128, n, 64], F32)
            self.omka = P.sb("omka", [64, n], F32)
            self.ld("c1", self.rwp[:, 0:n * 10 + 2], self.d_rwp[:, :], ["rwp"])
            self.ld("c2", self.rwpg[:, :], self.d_rwpg[:, :], ["rwpg"])
            self.ld("c3", self.rww2[:, :, :], self.d_rww2[:, :, :], ["rww2"])
            self.ld("c4", self.rwa2[:, :, :], self.d_rwa2[:, :, :], ["rwa2"])
            self.ld("c5", self.rwg2[:, :, :], self.d_rwg2[:, :, :], ["rwg2"])
            for i in range(n):
                self.ts("dve", self.omka[:, i:i + 1], self.rwp[:, i * 10 + 6:i * 10 + 7], -1.0, 1.0, ALU.mult, ALU.add,
                        ["rwp"], [("omka", i)])
                self.units.append(self._mk_unit(uid, "rw", i, 64, 64))
                uid += 1
        if self.n_gd:
            n = self.n_gd
            self.gdcw = P.sb("gdcw", [128, n * 12], F32)
            self.gdp = P.sb("gdp", [128, n * 3 + n], F32)
            self.ld("c6", self.gdcw[:, :], self.d_gdcw[:, :], ["gdcw"])
            self.ld("c7", self.gdp[:, 0:n * 3], self.d_gdp[:, :], ["gdp"])
            for i in range(n):
                self.av(self.gdp[:, n * 3 + i:n * 3 + i + 1], self.gdp[:, i * 3:i * 3 + 1], AF.Exp, ["gdp"], [("gdnea", i)])
                self.ts("dve", self.gdp[:, n * 3 + i:n * 3 + i + 1], self.gdp[:, n * 3 + i:n * 3 + i + 1], -1.0, None,
                        ALU.mult, None, [("gdnea", i)], [("gdnea", i)])
                self.units.append(self._mk_unit(uid, "gd", i, 128, 128))
                uid += 1
        if self.do_lr:
            self.lrw = P.sb("lrw", [128, 2, 128], F32)
            self.lrp = P.sb("lrp", [128, 12], F32)
            self.lrh = P.sb("lrh", [128, 2], F32)
            self.ld("c8", self.lrw[:, :, :], self.d_lrw[:, :, :], ["lrw"])
            self.ld("c9", self.lrp[:, 0:8], self.d_lrp[:, :], ["lrp"])
            self.av(self.lrp[:, 8:9], self.lrp[:, 7:8], AF.Exp, ["lrp"], ["lrp8"], scale=-1.0)
            self.av(self.lrp[:, 9:10], self.lrp[:, 8:9], AF.Ln, ["lrp8"], ["lrp9"], bias=self.one1)
            self.ts("dve", self.lrp[:, 10:11], self.lrp[:, 9:10], -8.0, None, ALU.mult, None, ["lrp9"], ["lrp10"])
            P.op("pool", lambda e: e.memset(self.lrh[:, :], 0.0), (), ["lrh"])

    def _mk_unit(self, uid, kind, idx, K, Vd):
        P = self.P
        u = {"id": uid, "kind": kind, "idx": idx, "K": K, "V": Vd}
        u["H"] = P.sb("H%d" % uid, [K, Vd], F32)
        P.op("pool", lambda e: e.memset(u["H"][:, :], 0.0), (), [("H", uid)])
        u["AT"] = P.sb("AT%d" % uid, [128, 512], F32)
        u["KV"] = P.sb("KV%d" % uid, [128, 384], F32)
        u["X2"] = P.sb("X2%d" % uid, [128, 2, 128], F32)
        u["N"] = P.sb("N%d" % uid, [128, 128], F32)
        u["Pp"] = [P.sb("Pp%d_%d" % (uid, i), [128, 128], F32) for i in range(2)]
        u["Pt"] = [P.sb("Pt%d_%d" % (uid, i), [128, 128], F32) for i in range(2)]
        u["Tt"] = [P.sb("Tt%d_%d" % (uid, i), [128, 128], F32) for i in range(2)]
        u["W1"] = P.sb("W1%d" % uid, [128, 128], F32)
        u["U"] = P.sb("U%d" % uid, [128, 128], F32)
        return u

    def core_stages(self, u, KKgT, RgT, gC, yT, Rin, yres):
        uid, K, Vd = u["id"], u["K"], u["V"]
        AT, KV, H = u["AT"], u["KV"], u["H"]
        rAT, rKV, rH = ("AT", uid), ("KV", uid), ("H", uid)
        st = []
        I, = (self.ident,)

        def chain_reg():
            bk, rk = self.newbank()
            return bk[:, 0:128], rk

        def s0():
            rg, rk = chain_reg()
            self.mm(rg, AT[:, 256:384], I, [rAT, "cst"], [rk])
            self.P.op("act", lambda e: e.activation(out=u["N"][:, :], in_=rg, func=AF.Copy), [rk], [("N", uid)])
            self.tt("pool", u["Tt"][0][:, :], I, AT[:, 256:384], ALU.subtract, [rAT, "cst"], [("Tt", uid, 0)])
        st.append(s0)
        for j in range(1, 7):
            def lv(j=j):
                a, b = (j - 1) % 2, j % 2
                Pprev = u["N"] if j == 1 else u["Pp"][a]
                rPprev = ("N", uid) if j == 1 else ("Pp", uid, a)
                Ptprev = AT[:, 256:384] if j == 1 else u["Pt"][a][:, :]
                rPtprev = rAT if j == 1 else ("Pt", uid, a)
                rg, rk = chain_reg()
                self.mm(rg, Ptprev, Pprev[:, :], [rPprev, rPtprev], [rk])
                self.P.op("act", lambda e: e.activation(out=u["Pp"][b][:, :], in_=rg, func=AF.Copy), [rk], [("Pp", uid, b)])
                if j < 6:
                    rg2, rk2 = chain_reg()
                    self.mm(rg2, Pprev[:, :], Ptprev, [rPprev, rPtprev], [rk2])
                    self.P.op("dve", lambda e: e.tensor_copy(out=u["Pt"][b][:, :], in_=rg2), [rk2], [("Pt", uid, b)])
            st.append(lv)

            def up(j=j):
                a, b = (j - 1) % 2, j % 2
                rg, rk = chain_reg()
                self.mm(rg, u["Pp"][b][:, :], u["Tt"][a][:, :], [("Pp", uid, b), ("Tt", uid, a)], [rk])
                self.tt("dve", u["Tt"][b][:, :], rg, u["Tt"][a][:, :], ALU.add, [rk, ("Tt", uid, a)], [("Tt", uid, b)])
            st.append(up)
        Tt, rTt = u["Tt"][0], ("Tt", uid, 0)
        V = KV[:, 2 * K:2 * K + Vd]

        def sW1():
            bk, rW1p = self.newbank()
            W1p = bk[:, 0:Vd]
            self.mm(W1p, KKgT, H[:, :], Rin + [rH], [rW1p], start=True, stop=False)
            self.mm(W1p, AT[:, 0:128], V, [rAT, rKV], [rW1p], start=False, stop=True)
            self.P.op("act", lambda e: e.activation(out=u["W1"][:, 0:Vd], in_=W1p, func=AF.Copy), [rW1p], [("W1", uid)])
        st.append(sW1)

        def sU():
            bk, rUp = self.newbank()
            Up = bk[:, 0:Vd]
            self.mm(Up, Tt[:, :], u["W1"][:, 0:Vd], [rTt, ("W1", uid)], [rUp])
            self.P.op("act", lambda e: e.activation(out=u["U"][:, 0:Vd], in_=Up, func=AF.Copy), [rUp], [("U", uid)])
        st.append(sU)

        def sYH():
            bk, rYp = self.newbank()
            Yp = bk[0:Vd, 0:128]
            bk2, rHp = self.newbank()
            Hp = bk2[0:K, 0:Vd]
            self.mm(Yp, H[:, :], RgT, Rin + [rH], [rYp], start=True, stop=False)
            self.mm(Yp, V, AT[:, 128:256], [rAT, rKV], [rYp], start=False, stop=False)
            self.mm(Yp, u["U"][:, 0:Vd], AT[:, 384:512], [rAT, ("U", uid)], [rYp], start=False, stop=True)
            self.P.op("act", lambda e: e.activation(out=yT, in_=Yp, func=AF.Copy), [rYp], [yres])
            self.mm(Hp, KV[:, 0:K], V, [rKV], [rHp], start=True, stop=False)
            self.mm(Hp, KV[:, K:2 * K], u["U"][:, 0:Vd], [rKV, ("U", uid)], [rHp], start=False, stop=True)
            self.stt(H[:, :], H[:, :], gC, Hp, ALU.mult, ALU.add, Rin + [rH, rHp], [rH])
        st.append(sYH)
        return st

    def _block(self, blk):
        TB, NCH = self.TB, self.NCH
        t0 = blk * TB
        preps = []
        for u in self.units:
            if u["kind"] == "rw":
                preps.append(self._rw_prep(u, blk))
            else:
                preps.append(self._gd_prep(u, blk))
        if self.do_lr:
            self._lru(blk)
        for c in range(NCH):
            lists = [pf(c) for pf in preps]
            if self.STAGE_LIMIT is not None:
                lists = [l[:self.STAGE_LIMIT] for l in lists]
            if not self.INTERLEAVE:
                for l in lists:
                    for fn in l:
                        fn()
                continue
            n = max(len(l) for l in lists) if lists else 0
            for k in range(n):
                for l in lists:
                    if k < len(l):
                        l[k]()
        for u in self.units:
            u["post"]()

    def Sx(self, k, p=128, w=None):
        w = self.TB if w is None else w
        return self.S[k][0:p, 0:w]

    def newbank(self):
        b = self.nxt("bank", 7)
        return self.psum[:, b * 512:(b + 1) * 512], ("bank", b)

    def _rw_shared(self, blk):
        TB = self.TB
        t0 = blk * TB
        n = self.n_rw
        S = self.S
        R = lambda k: ("S", k)
        self.ld("rl0", S[0][0:64, 0:TB + 1], self.d_rwl[0:64, t0:t0 + TB + 1], [R(0)])
        self.ld("rl1", S[1][0:64, 0:TB + 1], self.d_rwl[64:128, t0:t0 + TB + 1], [R(1)])
        self.ld("rl2", S[2][0:128, 0:TB + 1], self.d_rwl[128:256, t0:t0 + TB + 1], [R(2)])

        def shift(dst, dk, src, sk, mu, p, mures):
            self.tt("dve", S[3][0:p, 0:TB], S[src][0:p, 0:TB], S[src][0:p, 1:TB + 1], ALU.subtract, [R(src)], [R(3)])
            self.stt(S[dst][0:p, 0:TB], S[3][0:p, 0:TB], mu, S[src][0:p, 1:TB + 1], ALU.mult, ALU.add,
                     [R(3), R(src), mures], [R(dst)])
        self.shift = shift
        shift(4, 4, 0, 0, self.rwp[:, n * 10:n * 10 + 1], 64, "rwp")
        self.av(S[4][0:64, 0:TB], S[4][0:64, 0:TB], AF.Tanh, [R(4)], [R(4)])
        shift(5, 5, 1, 1, self.rwp[:, n * 10 + 1:n * 10 + 2], 64, "rwp")
        shift(6, 6, 2, 2, self.rwpg[:, 0:1], 128, "rwpg")
        self.av(S[6][0:128, 0:TB], S[6][0:128, 0:TB], AF.Sigmoid, [R(6)], [R(6)])

    def _rw_prep(self, u, blk):
        TB, NCH = self.TB, self.NCH
        t0 = blk * TB
        i, uid = u["idx"], u["id"]
        S = self.S
        R = lambda k: ("S", k)
        P = self.P
        if i == 0:
            self._rw_shared(blk)
        if "XT" not in u:
            u["XT"] = P.sb("XT%d" % uid, [64, 2, TB], F32)
            u["gC"] = P.sb("gC%d" % uid, [64, NCH], F32)
        XT, gCt = u["XT"], u["gC"]
        rXT, rgC = ("XT", uid), ("gCt", uid)
        pc = lambda c: self.rwp[:, i * 10 + c:i * 10 + c + 1]
        b0 = 21 + i * 8
        sYk, sYb, sKt, sBt, sV, sBon, sG, sY = range(b0, b0 + 8)
        X = lambda k: S[k][0:64, 0:TB]
        for m, tmp in ((0, 7), (1, 8), (2, 9)):
            self.ld("rx%d" % m, S[tmp][0:64, 0:TB + 1], self.d_rwx[i * 3 + m, :, t0:t0 + TB + 1], [R(tmp)])
        self.shift(10, 10, 7, 7, pc(0), 64, "rwp")
        self.shift(11, 11, 8, 8, pc(1), 64, "rwp")
        self.shift(sV, sV, 9, 9, pc(2), 64, "rwp")
        ps = self.psum[:, 7 * 512:8 * 512]
        rps = ("ps", 7)
        self.mm(ps[0:64, 0:TB], self.rww2[:, i, :], X(4), [R(4), "rww2"], [rps])
        self.av(X(12), ps[0:64, 0:TB], AF.Sigmoid, [rps, "rwp"], [R(12)], bias=pc(3))
        self.mm(ps[0:64, 0:TB], self.rwa2[:, i, :], X(5), [R(5), "rwa2"], [rps])
        self.av(X(13), ps[0:64, 0:TB], AF.Sigmoid, [rps, "rwp"], [R(13)], bias=pc(4))
        self.mm(ps[0:64, 0:TB], self.rwg2[:, i, :], S[6][0:128, 0:TB], [R(6), "rwg2"], [rps])
        self.av(X(sG), ps[0:64, 0:TB], AF.Copy, [rps], [R(sG)])
        self.ts("dve", X(14), X(11), pc(5), None, ALU.mult, None, [R(11), "rwp"], [R(14)])
        self.av(X(15), X(14), AF.Square, [R(14)], [R(15)])
        self.mm(ps[0:64, 0:TB], self.ones[0:64, 0:64], X(15), [R(15), "cst"], [rps])
        self.av(X(15), ps[0:64, 0:TB], AF.Sqrt, [rps, ("small", 0)], [R(15)], bias=self.eps12[0:64, :])
        self.rcp(X(15), X(15), [R(15)], [R(15)])
        self.tt("dve", X(14), X(14), X(15), ALU.mult, [R(14), R(15)], [R(14)])
        self.ts("dve", X(16), X(13), pc(6), self.omka[:, i:i + 1], ALU.mult, ALU.add, [R(13), "rwp", ("omka", i)], [R(16)])
        self.tt("dve", X(16), X(11), X(16), ALU.mult, [R(11), R(16)], [R(16)])
        self.tt("pool", X(17), X(14), X(13), ALU.mult, [R(14), R(13)], [R(17)])
        self.stt(X(15), X(10), pc(7), X(16), ALU.mult, ALU.mult, [R(10), R(16), "rwp"], [R(15)])
        self.mm(ps[0:64, 0:TB], self.ones[0:64, 0:64], X(15), [R(15), "cst"], [rps])
        self.tt("dve", X(sBon), ps[0:64, 0:TB], X(sV), ALU.mult, [rps, R(sV)], [R(sBon)])
        P.op("dve", lambda e: e.tensor_tensor_scan(out=X(18), data0=self.rowm[0:64, 0:TB], data1=X(12), initial=0.0,
                                                   op0=ALU.mult, op1=ALU.add), [R(12), "cst"], [R(18)])
        c = DEC_C
        self.tt("pool", X(19), X(18), X(12), ALU.subtract, [R(18), R(12)], [R(19)])
        self.av(X(19), X(19), AF.Exp, [R(19)], [R(19)], scale=-c)
        self.tt("dve", XT[:, 0, :], X(14), X(19), ALU.mult, [R(14), R(19)], [rXT])
        self.av(X(19), X(18), AF.Exp, [R(18), R(19)], [R(19)], scale=-c)
        self.tt("dve", XT[:, 1, :], X(10), X(19), ALU.mult, [R(10), R(19)], [rXT])
        self.av(X(19), X(18), AF.Exp, [R(18)], [R(19)], scale=c)
        self.tt("dve", X(sYk), X(16), X(19), ALU.mult, [R(16), R(19)], [R(sYk)])
        self.tt("pool", X(sYb), X(17), X(19), ALU.mult, [R(17), R(19)], [R(sYb)])
        Sc3 = X(18).rearrange("p (c t) -> p c t", t=CH)
        last = Sc3[:, :, CH - 1:CH].to_broadcast([64, NCH, CH])
        self.tt("dve", X(20).rearrange("p (c t) -> p c t", t=CH), last, Sc3, ALU.subtract, [R(18)], [R(20)])
        self.av(X(20), X(20), AF.Exp, [R(20)], [R(20)], scale=-c)
        self.tt("dve", X(sKt), X(16), X(20), ALU.mult, [R(16), R(20)], [R(sKt)])
        self.stt(X(sBt), X(17), -1.0, X(20), ALU.mult, ALU.mult, [R(17), R(20)], [R(sBt)])
        self.av(gCt[:, :], S[18][0:64, CH - 1:TB:CH], AF.Exp, [R(18)], [rgC], scale=-c)

        def chunk(cc):
            cs = slice(cc * CH, (cc + 1) * CH)
            AT, KV = u["AT"], u["KV"]

            def pA():
                bk, rb = self.newbank()
                o1 = bk[:, 0:256].rearrange("p (a t) -> p a t", a=2)
                o2 = bk[:, 256:512].rearrange("p (a t) -> p a t", a=2)
                self.mm(o1, S[sYk][0:64, cs], XT[:, :, cs], [R(sYk), rXT], [rb])
                self.mm(o2, S[sYb][0:64, cs], XT[:, :, cs], [R(sYb), rXT], [rb])
                self.tt("dve", AT[:, :], bk, self.mask4, ALU.mult, [rb, "cst"], [("AT", uid)])

            def pT():
                bkf, rb = self.newbank()
                bk = bkf[:, 0:192]
                self.mm(bk[:, 0:64], S[sKt][0:64, cs], self.ident[0:64, 0:64], [R(sKt), "cst"], [rb])
                self.mm(bk[:, 64:128], S[sBt][0:64, cs], self.ident[0:64, 0:64], [R(sBt), "cst"], [rb])
                self.mm(bk[:, 128:192], S[sV][0:64, cs], self.ident[0:64, 0:64], [R(sV), "cst"], [rb])
                P.op("act", lambda e: e.activation(out=KV[:, 0:192], in_=bk, func=AF.Copy), [rb], [("KV", uid)])
            core = self.core_stages(u, XT[:, 0, cs], XT[:, 1, cs], gCt[:, cc:cc + 1], S[sY][0:64, cs],
                                    [rXT, rgC], R(sY))
            return [pA, pT] + core

        def post():
            y = X(sY)
            self.mm(ps[0:64, 0:TB], self.ones[0:64, 0:64], y, [R(sY), "cst"], [rps])
            self.av(X(7), ps[0:64, 0:TB], AF.Copy, [rps], [R(7)], scale=1.0 / 64)
            self.av(X(8), y, AF.Square, [R(sY)], [R(8)])
            self.mm(ps[0:64, 0:TB], self.ones[0:64, 0:64], X(8), [R(8), "cst"], [rps])
            self.tt("pool", X(8), X(7), X(7), ALU.mult, [R(7)], [R(8)])
            self.stt(X(9), ps[0:64, 0:TB], 1.0 / 64, X(8), ALU.mult, ALU.subtract, [rps, R(8)], [R(9)])
            self.av(X(9), X(9), AF.Sqrt, [R(9), ("small", 1)], [R(9)], bias=self.epsgn[0:64, :])
            self.rcp(X(9), X(9), [R(9)], [R(9)])
            self.tt("dve", X(7), y, X(7), ALU.subtract, [R(sY), R(7)], [R(7)])
            self.tt("dve", X(7), X(7), X(9), ALU.mult, [R(7), R(9)], [R(7)])
            self.ts("dve", X(7), X(7), pc(8), pc(9), ALU.mult, ALU.add, [R(7), "rwp"], [R(7)])
            self.tt("pool", X(7), X(7), X(sBon), ALU.add, [R(7), R(sBon)], [R(7)])
            self.tt("dve", X(7), X(7), X(sG), ALU.mult, [R(7), R(sG)], [R(7)])
            self.st("ro%d" % i, self.d_rwo[i, :, t0:t0 + TB], X(7), [R(7)])
        u["post"] = post
        return chunk

    def _gd_prep(self, u, blk):
        TB, NCH = self.TB, self.NCH
        t0 = blk * TB
        i, uid = u["idx"], u["id"]
        n = self.n_gd
        S = self.S
        R = lambda k: ("S", k)
        P = self.P
        if "KQ" not in u:
            u["KQ"] = P.sb("KQ%d" % uid, [128, 2, TB], F32)
            u["smt"] = P.sb("smt%d" % uid, [128, 10, NCH], F32)
            u["row"] = P.sb("row%d" % uid, [1, 3, TB], F32)
        KQ, smt, row = u["KQ"], u["smt"], u["row"]
        rKQ, rsm, rrow = ("KQ", uid), ("smt", uid), ("row", uid)
        b0 = 45 + i * 4
        sVt, sGate, sY = b0, b0 + 1, b0 + 2
        X = lambda k: S[k][0:128, 0:TB]
        ps = self.psum[:, 7 * 512:8 * 512]
        rps = ("ps", 7)
        cw = lambda m, j: self.gdcw[:, i * 12 + m * 4 + j:i * 12 + m * 4 + j + 1]
        dtb = self.gdp[:, i * 3 + 1:i * 3 + 2]
        nea = self.gdp[:, n * 3 + i:n * 3 + i + 1]
        normg = self.gdp[:, i * 3 + 2:i * 3 + 3]
        dests = {0: KQ[:, 1, :], 1: KQ[:, 0, :], 2: X(sVt)}
        dres = {0: rKQ, 1: rKQ, 2: R(sVt)}
        for m in range(3):
            tmp = 7 + m
            self.ld("gx%d" % m, S[tmp][0:128, 0:TB + 3], self.d_gdx[i * 4 + m, :, t0:t0 + TB + 3], [R(tmp)])
            acc = X(10)
            self.ts("dve", acc, S[tmp][0:128, 0:TB], cw(m, 0), None, ALU.mult, None, [R(tmp), "gdcw"], [R(10)])
            for j in range(1, 4):
                self.stt(acc, S[tmp][0:128, j:j + TB], cw(m, j), acc, ALU.mult, ALU.add, [R(tmp), R(10), "gdcw"], [R(10)])
            if m == 2:
                self.av(dests[m], acc, AF.Silu, [R(10)], [dres[m]])
            else:
                self.av(X(11), acc, AF.Silu, [R(10)], [R(11)])
                self.av(X(12), X(11), AF.Square, [R(11)], [R(12)])
                self.mm(ps[:, 0:TB], self.ones, X(12), [R(12), "cst"], [rps])
                self.av(X(12), ps[:, 0:TB], AF.Sqrt, [rps, ("small", 0)], [R(12)], bias=self.eps12)
                self.rcp(X(12), X(12), [R(12)], [R(12)])
                sc = (128.0 ** -0.5) if m == 0 else 1.0
                self.stt(dests[m], X(11), sc, X(12), ALU.mult, ALU.mult, [R(11), R(12)], [dres[m]])
        self.ld("gx3", S[13][0:128, 0:TB], self.d_gdx[i * 4 + 3, :, t0 + 3:t0 + 3 + TB], [R(13)])
        self.av(X(sGate), X(13), AF.Silu, [R(13)], [R(sGate)])
        c0 = blk * NCH
        self.ld("gs0", smt[:, 0, :], self.d_gds[i * 2 + 0, :, c0:c0 + NCH], [rsm])
        self.ld("gs1", smt[:, 1, :], self.d_gds[i * 2 + 1, :, c0:c0 + NCH], [rsm])
        self.av(smt[:, 0, :], smt[:, 0, :], AF.Sigmoid, [rsm], [rsm])
        self.av(smt[:, 1, :], smt[:, 1, :], AF.Exp, [rsm, "gdp"], [rsm], bias=dtb)
        self.av(smt[:, 1, :], smt[:, 1, :], AF.Ln, [rsm, ("small", 3)], [rsm], bias=self.one1)
        self.ts("dve", smt[:, 2, :], smt[:, 1, :], nea, None, ALU.mult, None, [rsm, ("gdnea", i)], [rsm])
        pss = self.psum[:, 7 * 512:7 * 512 + 2 * NCH]
        self.mm(pss[:, 0:NCH], self.mask4[:, 128:256], smt[:, 2, :], [rsm, "cst"], [rps])
        self.mm(pss[:, NCH:2 * NCH], self.ones, smt[:, 2, :], [rsm, "cst"], [rps])
        self.av(smt[:, 3:5, :], pss.rearrange("p (a c) -> p a c", a=2), AF.Copy, [rps], [rsm])
        self.av(smt[:, 5, :], smt[:, 2, :], AF.Exp, [rsm], [rsm])
        self.tt("dve", smt[:, 6, :], smt[:, 4, :], smt[:, 3, :], ALU.subtract, [rsm], [rsm])
        self.av(smt[:, 6, :], smt[:, 6, :], AF.Exp, [rsm], [rsm])
        self.tt("dve", smt[:, 6, :], smt[:, 6, :], smt[:, 0, :], ALU.mult, [rsm], [rsm])
        self.stt(smt[:, 7, :], smt[:, 6, :], -1.0, smt[:, 5, :], ALU.mult, ALU.mult, [rsm], [rsm])
        self.av(smt[:, 8, :], smt[:, 4, :], AF.Exp, [rsm], [rsm])
        self.ld("gr0", row[0:1, 0, :], self.d_gdsr[i, :, t0:t0 + TB], [rrow])
        self.av(row[0:1, 0, :], row[0:1, 0, :], AF.Exp, [rrow, "gdp"], [rrow], bias=dtb[0:1, :])
        self.av(row[0:1, 0, :], row[0:1, 0, :], AF.Ln, [rrow, ("small", 3)], [rrow], bias=self.one1[0:1, :])
        self.ts("dve", row[0:1, 0, :], row[0:1, 0, :], nea[0:1, :], None, ALU.mult, None, [rrow, ("gdnea", i)], [rrow])
        P.op("dve", lambda e: e.tensor_tensor_scan(out=row[0:1, 2, :], data0=self.rowm[0:1, 0:TB], data1=row[0:1, 0, :],
                                                   initial=0.0, op0=ALU.mult, op1=ALU.add), [rrow, "cst"], [rrow])
        self.tt("dve", row[0:1, 1, :], row[0:1, 2, :], row[0:1, 0, :], ALU.subtract, [rrow], [rrow])

        def chunk(cc):
            cs = slice(cc * CH, (cc + 1) * CH)
            AT, KV, X2 = u["AT"], u["KV"], u["X2"]
            rAT, rKV, rX2 = ("AT", uid), ("KV", uid), ("X2", uid)
            col = lambda k: smt[:, k, cc:cc + 1]
            st = {}

            def pA():
                bk, rb = self.newbank()
                st["bk"], st["rb"] = bk, rb
                o1 = bk[:, 0:256].rearrange("p (a t) -> p a t", a=2)
                o2 = bk[:, 256:512].rearrange("p (a t) -> p a t", a=2)
                self.mm(o1, KQ[:, 0, cs], KQ[:, :, cs], [rKQ], [rb])
                self.mm(o2, self.ones[0:1, :], row[0:1, 1:3, cs], [rrow, "cst"], [rb])
                self.ts("dve", AT[:, 256:512], bk[:, 256:512], col(3), 0.0, ALU.subtract, ALU.min, [rb, rsm], [rAT])
                self.av(AT[:, 256:512], AT[:, 256:512], AF.Exp, [rAT], [rAT])
                self.av(X2[:, :, :], o2, AF.Exp, [rb], [rX2])

            def pB():
                bk, rb = st["bk"], st["rb"]
                self.stt(AT[:, 256:512], AT[:, 256:512], col(0), self.mask4[:, 0:256], ALU.mult, ALU.mult,
                         [rAT, rsm, "cst"], [rAT])
                self.tt("dve", AT[:, 0:256], bk[:, 0:256], AT[:, 256:512], ALU.mult, [rb, rAT], [rAT])
                self.ts("dve", AT[:, 256:384], AT[:, 0:128], col(5), None, ALU.mult, None, [rAT, rsm], [rAT])
                self.ts("dve", AT[:, 384:512], AT[:, 128:256], col(5), -1.0, ALU.mult, ALU.mult, [rAT, rsm], [rAT])
                self.tt("pool", X2[:, :, :], KQ[:, :, cs], X2[:, :, :], ALU.mult, [rKQ, rX2], [rX2])

            def pT():
                bkf, rb = self.newbank()
                bk = bkf[:, 0:256]
                self.mm(bk[:, 0:128], KQ[:, 0, cs], self.ident, [rKQ, "cst"], [rb])
                self.mm(bk[:, 128:256], S[sVt][0:128, cs], self.ident, [R(sVt), "cst"], [rb])
                self.ts("dve", KV[:, 0:128], bk[:, 0:128], col(6), None, ALU.mult, None, [rb, rsm], [rKV])
                self.ts("dve", KV[:, 128:256], bk[:, 0:128], col(7), None, ALU.mult, None, [rb, rsm], [rKV])
                P.op("act", lambda e: e.activation(out=KV[:, 256:384], in_=bk[:, 128:256], func=AF.Copy), [rb], [rKV])
            core = self.core_stages(u, X2[:, 0, :], X2[:, 1, :], col(8), S[sY][0:128, cs], [rX2, rsm], R(sY))
            return [pA, pB, pT] + core

        def post():
            y = X(sY)
            self.av(X(7), y, AF.Square, [R(sY)], [R(7)])
            self.mm(ps[:, 0:TB], self.ones, X(7), [R(7), "cst"], [rps])
            self.av(X(7), ps[:, 0:TB], AF.Sqrt, [rps, ("small", 2)], [R(7)], scale=1.0 / 128, bias=self.epsrms)
            self.rcp(X(7), X(7), [R(7)], [R(7)])
            self.tt("dve", X(7), y, X(7), ALU.mult, [R(sY), R(7)], [R(7)])
            self.stt(X(7), X(7), normg, X(sGate), ALU.mult, ALU.mult, [R(7), R(sGate), "gdp"], [R(7)])
            self.st("go%d" % i, self.d_gdo[i, :, t0:t0 + TB], X(7), [R(7)])
        u["post"] = post
        return chunk

    def _lru(self, blk):
        TB = self.TB
        t0 = blk * TB
        S = self.S
        R = lambda k: ("S", k)
        P = self.P
        X = lambda k: S[k][0:128, 0:TB]
        pcol = lambda k: self.lrp[:, k:k + 1]
        ps = self.psum[:, 7 * 512:8 * 512]
        rps = ("ps", 7)
        self.ld("lx0", S[7][0:128, 0:TB + 3], self.d_lrx[0, :, t0:t0 + TB + 3], [R(7)])
        self.ld("lx1", S[8][0:128, 0:TB], self.d_lrx[1, :, t0 + 3:t0 + 3 + TB], [R(8)])
        xc = X(9)
        self.ts("dve", xc, S[7][0:128, 0:TB], pcol(0), pcol(4), ALU.mult, ALU.add, [R(7), "lrp"], [R(9)])
        for j in range(1, 4):
            self.stt(xc, S[7][0:128, j:j + TB], pcol(j), xc, ALU.mult, ALU.add, [R(7), R(9), "lrp"], [R(9)])
        self.mm(ps[:, 0:TB], self.lrw[:, 0, :], xc, [R(9), "lrw"], [rps])
        self.av(X(10), ps[:, 0:TB], AF.Sigmoid, [rps, "lrp"], [R(10)], bias=pcol(5))
        self.mm(ps[:, 0:TB], self.lrw[:, 1, :], xc, [R(9), "lrw"], [rps])
        self.av(X(11), ps[:, 0:TB], AF.Sigmoid, [rps, "lrp"], [R(11)], bias=pcol(6))
        self.av(X(10), X(10), AF.Exp, [R(10), "lrp10"], [R(10)], scale=self.lrp[:, 10:11])
        self.tt("pool", X(12), X(10), X(10), ALU.mult, [R(10)], [R(12)])
        self.ts("dve", X(12), X(12), -1.0, 1.0, ALU.mult, ALU.add, [R(12)], [R(12)])
        self.av(X(12), X(12), AF.Sqrt, [R(12)], [R(12)])
        self.tt("dve", X(11), X(11), xc, ALU.mult, [R(11), R(9)], [R(11)])
        self.tt("dve", X(11), X(11), X(12), ALU.mult, [R(11), R(12)], [R(11)])
        hb = blk % 2
        P.op("dve", lambda e: e.tensor_tensor_scan(out=X(12), data0=X(10), data1=X(11),
                                                   initial=self.lrh[:, hb:hb + 1], op0=ALU.mult, op1=ALU.add),
             [R(10), R(11), ("lrh", hb)], [R(12)])
        P.op("act", lambda e: e.activation(out=self.lrh[:, 1 - hb:2 - hb], in_=S[12][0:128, TB - 1:TB], func=AF.Copy),
             [R(12)], [("lrh", 1 - hb)])
        y = X(8)
        self.av(X(9), y, AF.Square, [R(8)], [R(9)])
        self.ts("dve", X(9), X(9), 0.044715, 1.0, ALU.mult, ALU.add, [R(9)], [R(9)])
        self.tt("pool", X(9), X(9), y, ALU.mult, [R(9), R(8)], [R(9)])
        self.av(X(9), X(9), AF.Sigmoid, [R(9)], [R(9)], scale=1.5957691216057308)
        self.tt("dve", X(9), X(9), y, ALU.mult, [R(9), R(8)], [R(9)])
        self.tt("dve", X(9), X(9), X(12), ALU.mult, [R(9), R(12)], [R(9)])
        self.st("lo", self.d_lro[:, t0:t0 + TB], X(9), [R(9)])


RW_IN = 2560
GD_IN = 3084
RWKV_DIM = 768
GDN_DIM = 768
LRU_DIM = 512


def mixer_consts(TB):
    c = np.zeros((128, 128 + 128 + 512 + TB), np.float32)
    c[:, 0:128] = np.eye(128, dtype=np.float32)
    c[:, 128:256] = 1.0
    s = np.arange(128)[:, None]
    t = np.arange(128)[None, :]
    su = (s < t).astype(np.float32)
    ui = (s <= t).astype(np.float32)
    c[:, 256:384] = su
    c[:, 384:512] = ui
    c[:, 512:640] = su
    c[:, 640:768] = -ui
    rm = np.ones(TB, np.float32)
    rm[0::CH] = 0.0
    c[:, 768:] = rm[None, :]
    return c


def gd_heads(j):
    return [j, 4 + j if j < 2 else j]


def mixer_inputs(zb, p, l, j, T, TB):
    f = np.float32
    padl = lambda a, n: np.concatenate([np.zeros(a.shape[:-1] + (n,), f), a], axis=-1)
    m = {"consts": mixer_consts(TB)}
    heads = [3 * j + i for i in range(3)]
    rows = []
    for hd in heads:
        for q in range(3):
            rows.append(zb[q * 768 + hd * 64:q * 768 + (hd + 1) * 64])
    m["rw_x"] = np.ascontiguousarray(padl(np.stack(rows), 1))
    m["rw_l"] = np.ascontiguousarray(padl(zb[2304:2560], 1))
    mu = p["rw_mu"][l]
    cols = []
    for hd in heads:
        hs = slice(hd * 64, (hd + 1) * 64)
        cols += [mu[0:768][hs], mu[768:1536][hs], mu[1536:2304][hs], p["rw_w0"][l][hs], p["rw_a0"][l][hs],
                 p["rw_kk"][l][hs], p["rw_ka"][l][hs], p["rw_rk"][l][hd], p["rw_ln_g"][l][hs], p["rw_ln_b"][l][hs]]
    cols += [mu[2304:2368], mu[2368:2432]]
    m["rw_p"] = np.ascontiguousarray(np.stack(cols, axis=1).astype(f))
    m["rw_pg"] = np.ascontiguousarray(mu[2432:2560].reshape(128, 1).astype(f))
    hsel = lambda w: np.ascontiguousarray(np.stack([w[:, hd * 64:(hd + 1) * 64] for hd in heads], axis=1).astype(f))
    m["rw_w2"] = hsel(p["rw_w2"][l])
    m["rw_a2"] = hsel(p["rw_a2"][l])
    m["rw_g2"] = hsel(p["rw_g2"][l])
    zg = zb[RW_IN:RW_IN + GD_IN]
    gh = gd_heads(j)
    rows, cw, srow, stok, gp = [], [], [], [], []
    for h in gh:
        hs = slice(h * 128, (h + 1) * 128)
        for q in range(3):
            rows.append(padl(zg[q * 768:(q + 1) * 768][hs], 3))
            for tap in range(4):
                cw.append(p["gd_conv_w"][l][tap, q * 768:(q + 1) * 768][hs])
        rows.append(padl(zg[2304:3072][hs], 3))
        zbeta, za = zg[3072 + h], zg[3078 + h]
        stok += [zbeta.reshape(T // CH, CH).T, za.reshape(T // CH, CH).T]
        srow.append(za.reshape(1, T))
        gp += [np.full(128, p["gd_a_log"][l][h], f), np.full(128, p["gd_dt_bias"][l][h], f), p["gd_norm_g"][l]]
    m["gd_x"] = np.ascontiguousarray(np.stack(rows).astype(f))
    m["gd_cw"] = np.ascontiguousarray(np.stack(cw, axis=1).astype(f))
    m["gd_s"] = np.ascontiguousarray(np.stack(stok).astype(f))
    m["gd_sr"] = np.ascontiguousarray(np.stack(srow).astype(f))
    m["gd_p"] = np.ascontiguousarray(np.stack(gp, axis=1).astype(f))
    zl = zb[RW_IN + GD_IN:]
    cs = slice(128 * j, 128 * (j + 1))
    m["lr_x"] = np.ascontiguousarray(np.stack([padl(zl[0:512][cs], 3), padl(zl[512:1024][cs], 3)]).astype(f))
    lw = np.zeros((128, 2, 128), f)
    for q, nm in enumerate(("lr_wa", "lr_wx")):
        for bi in range(2):
            lw[bi * 64:(bi + 1) * 64, q, bi * 64:(bi + 1) * 64] = p[nm][l][2 * j + bi]
    m["lr_w"] = lw
    cw4 = p["lr_conv_w"][l][:, cs]
    m["lr_p"] = np.ascontiguousarray(np.stack([cw4[0], cw4[1], cw4[2], cw4[3], p["lr_conv_b"][l][cs], p["lr_ba"][l][cs],
                                               p["lr_bx"][l][cs], p["lr_lam"][l][cs]], axis=1).astype(f))
    return m


def mixer_scatter(mixT_b, res, j):
    for i in range(3):
        hd = 3 * j + i
        mixT_b[hd * 64:(hd + 1) * 64] = res["mix_rw"][i]
    gh = gd_heads(j)
    for i, h in enumerate(gh):
        if i == 1 and j >= 2:
            continue
        mixT_b[RWKV_DIM + h * 128:RWKV_DIM + (h + 1) * 128] = res["mix_gd"][i]
    mixT_b[RWKV_DIM + GDN_DIM + 128 * j:RWKV_DIM + GDN_DIM + 128 * (j + 1)] = res["mix_lr"]


def _run(nc, in_maps):
    res = run_bass_kernel_spmd(nc, in_maps, core_ids=list(range(len(in_maps))))
    return res.results


def kernel(**inputs):
    f = np.float32
    p = {k: np.asarray(v) for k, v in inputs.items()}
    x = p["x"]
    B, T, D = x.shape
    NSEG = NCORES // B
    Tc = T // NSEG
    TB = 512
    c_ = np.ascontiguousarray

    def ffn_in(si, gname, pre, l):
        return {"g%d" % si: g_pm(p[gname][l].astype(f)), "wg%d" % si: c_(p[pre + "_wg"][l]),
                "wu%d" % si: c_(p[pre + "_wu"][l]), "wd%d" % si: c_(p[pre + "_wd"][l])}

    def win_in(si, l):
        return {"g%d" % si: g_pm(p["norm_mix_g"][l].astype(f)), "win%d" % si: c_(p["w_in"][l])}

    def run_mixers(mx, zT_cores, l):
        ims = []
        for b in range(B):
            zb = np.concatenate([zT_cores[b * NSEG + s] for s in range(NSEG)], axis=1)
            for j in range(NSEG):
                ims.append(mixer_inputs(zb, p, l, j, T, TB))
        res = _run(mx.nc, ims)
        mixT = []
        for b in range(B):
            mb = np.zeros((D, T), f)
            for j in range(NSEG):
                mixer_scatter(mb, res[b * NSEG + j], j)
            mixT.append(mb)
        return mixT

    d1 = Dense(Tc, [("ffn",), ("win",)])
    ims = []
    for c in range(NCORES):
        b, s = divmod(c, NSEG)
        m = {"xT": c_(x[b, s * Tc:(s + 1) * Tc].T)}
        m.update(ffn_in(0, "norm1_g", "ffn1", 0))
        m.update(win_in(1, 0))
        ims.append(m)
    r1 = _run(d1.nc, ims)
    mx = Mixer(T, TB=TB)
    mix0 = run_mixers(mx, [r["zT"] for r in r1], 0)
    d3 = Dense(Tc, [("wout",), ("ffn",), ("ffn",), ("win",)])
    ims = []
    for c in range(NCORES):
        b, s = divmod(c, NSEG)
        m = {"xT": r1[c]["oT"], "mixT": c_(mix0[b][:, s * Tc:(s + 1) * Tc]), "wout0": c_(p["w_out"][0])}
        m.update(ffn_in(1, "norm2_g", "ffn2", 0))
        m.update(ffn_in(2, "norm1_g", "ffn1", 1))
        m.update(win_in(3, 1))
        ims.append(m)
    r3 = _run(d3.nc, ims)
    mix1 = run_mixers(mx, [r["zT"] for r in r3], 1)
    d5 = Dense(Tc, [("wout",), ("ffn",), ("final",)])
    ims = []
    for c in range(NCORES):
        b, s = divmod(c, NSEG)
        m = {"xT": r3[c]["oT"], "mixT": c_(mix1[b][:, s * Tc:(s + 1) * Tc]), "wout0": c_(p["w_out"][1])}
        m.update(ffn_in(1, "norm2_g", "ffn2", 1))
        m["g2"] = g_pm(p["final_g"].astype(f))
        ims.append(m)
    r5 = _run(d5.nc, ims)
    out = np.zeros((B, T, D), f)
    for c in range(NCORES):
        b, s = divmod(c, NSEG)
        out[b, s * Tc:(s + 1) * Tc] = r5[c]["oT"].T
    return out
```
